# Optimizing a Trainium2 kernel written in Bass

```python
import jax, jax.numpy as jnp
from jax import lax
import numpy as np

D_MODEL = 2048
BATCH = 1
SEQ = 16384
DEPTH = 1
DEC_BATCH = 4
DEC_SEQ = 4096
PAST_LEN = 128

PLE_DIM = 256
ROPE_THETA = 10000.0
EPS = 1e-6
BLOCK = 128
MLA_HEADS = 8
MLA_Q_LORA = 512
MLA_KV_LORA = 256
MLA_NOPE = 128
MLA_ROPE = 64
MLA_V = 128
MLA_QK = MLA_NOPE + MLA_ROPE
SWA_HEADS = 8
SWA_KV_HEADS = 2
SWA_GROUP = SWA_HEADS // SWA_KV_HEADS
SWA_HEAD_DIM = 128
WINDOW = 128
MIX_MLA = MLA_HEADS * MLA_V
MIX_SWA = SWA_HEADS * SWA_HEAD_DIM
MIX_WIDTH = MIX_MLA + MIX_SWA
D_FF = 4 * D_MODEL
IN_SIZES = (MLA_Q_LORA, MLA_KV_LORA, MLA_ROPE, SWA_HEADS * SWA_HEAD_DIM,
            SWA_KV_HEADS * SWA_HEAD_DIM, SWA_KV_HEADS * SWA_HEAD_DIM)
IN_WIDTH = sum(IN_SIZES)
IN_OFFSETS = tuple(int(v) for v in np.cumsum(IN_SIZES)[:-1])

kernel_name = "hybrid_mla_swa_sink_encoder"


def rms_norm(x, g):
    xf = x.astype(jnp.float32)
    y = xf * lax.rsqrt(jnp.mean(xf * xf, axis=-1, keepdims=True) + EPS)
    return (y * g.astype(jnp.float32)).astype(x.dtype)


def rope_tables(seq, dim):
    inv = 1.0 / (ROPE_THETA ** (jnp.arange(0, dim, 2, dtype=jnp.float32) / dim))
    ang = jnp.arange(seq, dtype=jnp.float32)[:, None] * inv[None, :]
    return jnp.cos(ang), jnp.sin(ang)


def apply_rope(x, cos, sin):
    xf = x.astype(jnp.float32)
    half = xf.shape[-1] // 2
    x1, x2 = xf[..., :half], xf[..., half:]
    c = cos[None, :, None, :]
    s = sin[None, :, None, :]
    return jnp.concatenate([x1 * c - x2 * s, x2 * c + x1 * s], axis=-1).astype(x.dtype)


def dense_attention(q, k, v, scale):
    b, s, h, d = q.shape
    nb = s // BLOCK
    qb = q.reshape(b, nb, BLOCK, h, d).transpose(1, 0, 2, 3, 4)

    def one_block(qblk):
        logits = jnp.einsum('bqhd,bkhd->bhqk', qblk, k).astype(jnp.float32) * scale
        probs = jax.nn.softmax(logits, axis=-1).astype(v.dtype)
        return jnp.einsum('bhqk,bkhd->bqhd', probs, v)

    o = lax.map(one_block, qb)
    return o.transpose(1, 0, 2, 3, 4).reshape(b, s, h * v.shape[-1])


def mla_mixer(c_q, c_kv, k_pe, g_qa, w_qb, g_kva, w_kvb, g_qn, g_kn):
    b, s, _ = c_q.shape
    q = (rms_norm(c_q, g_qa) @ w_qb).reshape(b, s, MLA_HEADS, MLA_QK)
    q = rms_norm(q, g_qn)
    kv = (rms_norm(c_kv, g_kva) @ w_kvb).reshape(b, s, MLA_HEADS, MLA_NOPE + MLA_V)
    k_nope, v = kv[..., :MLA_NOPE], kv[..., MLA_NOPE:]
    k_rope = jnp.broadcast_to(k_pe[:, :, None, :], (b, s, MLA_HEADS, MLA_ROPE))
    k = rms_norm(jnp.concatenate([k_nope, k_rope], axis=-1), g_kn)
    cos, sin = rope_tables(s, MLA_ROPE)
    q = jnp.concatenate([q[..., :MLA_NOPE], apply_rope(q[..., MLA_NOPE:], cos, sin)], axis=-1)
    k = jnp.concatenate([k[..., :MLA_NOPE], apply_rope(k[..., MLA_NOPE:], cos, sin)], axis=-1)
    return dense_attention(q, k, v, MLA_QK ** -0.5)


def window_sink_mixer(q, k, v, g_q, g_k, sink):
    b, s, _ = q.shape
    nb = s // BLOCK
    q = rms_norm(q.reshape(b, s, SWA_HEADS, SWA_HEAD_DIM), g_q)
    k = rms_norm(k.reshape(b, s, SWA_KV_HEADS, SWA_HEAD_DIM), g_k)
    v = v.reshape(b, s, SWA_KV_HEADS, SWA_HEAD_DIM)
    cos, sin = rope_tables(s, SWA_HEAD_DIM)
    q = apply_rope(q, cos, sin)
    k = apply_rope(k, cos, sin)
    pad = ((0, 0), (BLOCK, BLOCK), (0, 0), (0, 0))
    kp = jnp.pad(k, pad).reshape(b, nb + 2, BLOCK, SWA_KV_HEADS, SWA_HEAD_DIM)
    vp = jnp.pad(v, pad).reshape(b, nb + 2, BLOCK, SWA_KV_HEADS, SWA_HEAD_DIM)
    kw = jnp.concatenate([kp[:, :-2], kp[:, 1:-1], kp[:, 2:]], axis=2)
    vw = jnp.concatenate([vp[:, :-2], vp[:, 1:-1], vp[:, 2:]], axis=2)
    qg = q.reshape(b, nb, BLOCK, SWA_KV_HEADS, SWA_GROUP, SWA_HEAD_DIM)
    logits = jnp.einsum('bnqkgd,bnskd->bnkgqs', qg, kw).astype(jnp.float32) * (SWA_HEAD_DIM ** -0.5)
    blk = jnp.arange(nb)[:, None]
    qpos = blk * BLOCK + jnp.arange(BLOCK)[None, :]
    kpos = (blk - 1) * BLOCK + jnp.arange(3 * BLOCK)[None, :]
    mask = (jnp.abs(qpos[:, :, None] - kpos[:, None, :]) <= WINDOW) & (kpos[:, None, :] >= 0) & (kpos[:, None, :] < s)
    logits = jnp.where(mask[None, :, None, None, :, :], logits, -jnp.inf)
    sink_b = sink.astype(jnp.float32).reshape(SWA_KV_HEADS, SWA_GROUP)[None, None, :, :, None, None]
    m = jnp.maximum(jnp.max(logits, axis=-1, keepdims=True), sink_b)
    p = jnp.exp(logits - m)
    denom = jnp.sum(p, axis=-1, keepdims=True) + jnp.exp(sink_b - m)
    probs = (p / denom).astype(v.dtype)
    o = jnp.einsum('bnkgqs,bnskd->bnqkgd', probs, vw)
    return o.reshape(b, s, MIX_SWA)


def layer(x, p_l, g_attn, w_in, g_qa, w_qb, g_kva, w_kvb, g_qn_mla, g_kn_mla,
          g_q_swa, g_k_swa, sink, g_out_mla, g_out_swa, w_out, g_mlp, w_up, w_down,
          g_ple, w_ple, w_gate):
    h = rms_norm(x, g_attn)
    c_q, c_kv, k_pe, q_s, k_s, v_s = jnp.split(h @ w_in, IN_OFFSETS, axis=-1)
    o_a = rms_norm(mla_mixer(c_q, c_kv, k_pe, g_qa, w_qb, g_kva, w_kvb, g_qn_mla, g_kn_mla), g_out_mla)
    o_b = rms_norm(window_sink_mixer(q_s, k_s, v_s, g_q_swa, g_k_swa, sink), g_out_swa)
    x = x + jnp.concatenate([o_a, o_b], axis=-1) @ w_out
    hm = rms_norm(x, g_mlp)
    x = x + jnp.square(jax.nn.relu(hm @ w_up)) @ w_down
    gate = jax.nn.sigmoid(rms_norm(x, g_ple) @ w_gate)
    return x + (p_l @ w_ple) * gate


def setup_inputs(seed: int = 0) -> dict:
    key = jax.random.key(seed)
    ks = iter(jax.random.split(key, 32))

    def w(shape, fan_in):
        return jax.random.normal(next(ks), (DEPTH,) + shape, jnp.float32) * (fan_in ** -0.5)

    def g(dim):
        return 1.0 + 0.01 * jax.random.normal(next(ks), (DEPTH, dim), jnp.float32)

    return {
        "x_prompt": jax.random.normal(next(ks), (BATCH, SEQ, D_MODEL), jnp.float32),
        "x_sample": jax.random.normal(next(ks), (DEC_BATCH, DEC_SEQ, D_MODEL), jnp.float32),
        "p_prompt": jax.random.normal(next(ks), (DEPTH, BATCH, SEQ, PLE_DIM), jnp.float32),
        "p_sample": jax.random.normal(next(ks), (DEPTH, DEC_BATCH, DEC_SEQ, PLE_DIM), jnp.float32),
        "g_attn": g(D_MODEL),
        "w_in": w((D_MODEL, IN_WIDTH), D_MODEL),
        "g_qa": g(MLA_Q_LORA),
        "w_qb": w((MLA_Q_LORA, MLA_HEADS * MLA_QK), MLA_Q_LORA),
        "g_kva": g(MLA_KV_LORA),
        "w_kvb": w((MLA_KV_LORA, MLA_HEADS * (MLA_NOPE + MLA_V)), MLA_KV_LORA),
        "g_qn_mla": g(MLA_QK),
        "g_kn_mla": g(MLA_QK),
        "g_q_swa": g(SWA_HEAD_DIM),
        "g_k_swa": g(SWA_HEAD_DIM),
        "sink": 0.5 * jax.random.normal(next(ks), (DEPTH, SWA_HEADS), jnp.float32),
        "g_out_mla": g(MIX_MLA),
        "g_out_swa": g(MIX_SWA),
        "w_out": w((MIX_WIDTH, D_MODEL), MIX_WIDTH),
        "g_mlp": g(D_MODEL),
        "w_up": w((D_MODEL, D_FF), D_MODEL),
        "w_down": w((D_FF, D_MODEL), D_FF),
        "g_ple": g(D_MODEL),
        "w_ple": w((PLE_DIM, D_MODEL), PLE_DIM),
        "w_gate": w((D_MODEL, D_MODEL), D_MODEL),
    }


def reference(x_prompt, x_sample, p_prompt, p_sample, g_attn, w_in, g_qa, w_qb, g_kva, w_kvb,
              g_qn_mla, g_kn_mla, g_q_swa, g_k_swa, sink, g_out_mla, g_out_swa, w_out,
              g_mlp, w_up, w_down, g_ple, w_ple, w_gate):
    y_prompt = x_prompt
    y_sample = x_sample
    for l in range(DEPTH):
        params = (g_attn[l], w_in[l], g_qa[l], w_qb[l], g_kva[l], w_kvb[l], g_qn_mla[l], g_kn_mla[l],
                  g_q_swa[l], g_k_swa[l], sink[l], g_out_mla[l], g_out_swa[l], w_out[l],
                  g_mlp[l], w_up[l], w_down[l], g_ple[l], w_ple[l], w_gate[l])
        y_prompt = layer(y_prompt, p_prompt[l], *params)
        y_sample = layer(y_sample, p_sample[l], *params)
    return (y_prompt, y_sample)
```

```python
import numpy as np
from contextlib import ExitStack
import concourse.bass as bass
import concourse.mybir as mybir
from concourse.bass_utils import run_bass_kernel_spmd

F32 = mybir.dt.float32
BF16 = mybir.dt.bfloat16
AF = mybir.ActivationFunctionType
ALU = mybir.AluOpType
AX = mybir.AxisListType

D = 2048
DC = 16
EPS = 1e-6
NCORES = 8


class Buf:
    def __init__(self, name, psum=False):
        self.name = name
        self.psum = psum
        self.w = {}
        self.r = {}


class _Stop(Exception):
    pass


_LIMIT = [None]
_MARKS = []


class TR:
    nops = 0

    def tick(self):
        self.nops += 1
        if _LIMIT[0] is not None and self.nops > _LIMIT[0]:
            raise _Stop()

    def drain(self):
        for e in self.eng:
            if self.cnt[e] > 0 and e != "sp":
                self.eng["sp"].wait_ge(self.sem[e], self.cnt[e])
        for key, ent in self.dsem.items():
            if ent[1] > 0:
                self.eng["sp"].wait_ge(ent[0], ent[1])

    def __init__(self, nc, es):
        self.nc = nc
        self.es = es
        self.eng = {"pe": nc.tensor, "act": nc.scalar, "dve": nc.vector, "pool": nc.gpsimd, "sp": nc.sync}
        self.sem = {k: es.enter_context(nc.semaphore("c_" + k)) for k in self.eng}
        self.cnt = {k: 0 for k in self.eng}
        self.waited = {k: {} for k in self.eng}
        self.dsem = {}
        self.nds = 0

    def _need(self, e, evs):
        for key, (sem, val) in evs.items():
            if key == e and e == "pe":
                continue
            if self.waited[e].get(key, 0) < val:
                self.eng[e].wait_ge(sem, val)
                self.waited[e][key] = val

    def op(self, e, fn, r=(), w=(), inc=True):
        self.tick()
        if _LIMIT[0] is not None:
            inc = True
        for b in r:
            self._need(e, b.w)
            if b.psum:
                self._need(e, {k: v for k, v in b.r.items() if k != e})
        for b in w:
            self._need(e, b.w)
            self._need(e, b.r)
        ins = fn(self.eng[e])
        if inc:
            ins.then_inc(self.sem[e], 1)
            self.cnt[e] += 1
            val = self.cnt[e]
        else:
            val = self.cnt[e] + 1
        for b in r:
            b.r[e] = (self.sem[e], val)
        for b in w:
            b.w = {e: (self.sem[e], val)}
            b.r = {}
        return ins

    def dma(self, q, out, in_, r=(), w=(), sem_of=None, acc=False):
        self.tick()
        for b in r:
            self._need(q, b.w)
        for b in w:
            if not acc:
                self._need(q, b.w)
            self._need(q, b.r)
        key = "d_" + sem_of.name
        if key not in self.dsem:
            self.nds += 1
            self.dsem[key] = [self.es.enter_context(self.nc.semaphore("d%d" % self.nds)), 0]
        ent = self.dsem[key]
        self.eng[q].dma_start(out=out, in_=in_).then_inc(ent[0], 16)
        ent[1] += 16
        ev = (ent[0], ent[1])
        for b in r:
            b.r[key] = ev
        for b in w:
            if acc:
                b.w[key] = ev
            else:
                b.w = {key: ev}
                b.r = {}

    def barrier(self):
        for e in self.eng:
            for f in self.eng:
                if f != e and self.cnt[f] > self.waited[e].get(f, 0):
                    self.eng[e].wait_ge(self.sem[f], self.cnt[f])
                    self.waited[e][f] = self.cnt[f]
            for key, ent in self.dsem.items():
                if ent[1] > self.waited[e].get(key, 0):
                    self.eng[e].wait_ge(ent[0], ent[1])
                    self.waited[e][key] = ent[1]

    def finish(self, bufs):
        for b in bufs:
            self._need("sp", b.w)


_STOP = [None]


def build(S, OWN):
    holder = {}
    try:
        return _build(S, OWN, holder)
    except _Stop:
        holder['tr'].drain()
        return holder['nc'], holder['es']


def _build(S, OWN, holder):
    nc = bass.Bass("TRN2", target_bir_lowering=False)
    es = ExitStack()
    tr = TR(nc, es)
    holder.update(nc=nc, es=es, tr=tr)
    NG = 2
    OT = OWN // 128
    NB = OWN // 512

    def dram(name, shape, dt, kind):
        return nc.dram_tensor(name, list(shape), dt, kind=kind).ap()

    xkv = [dram("xkv%d" % g, [S[g], D], F32, "ExternalInput") for g in range(NG)]
    cskv = [dram("cskv%d" % g, [S[g], 64], F32, "ExternalInput") for g in range(NG)]
    xown = [dram("xown%d" % g, [OWN + 256, D], F32, "ExternalInput") for g in range(NG)]
    csown = [dram("csown%d" % g, [OWN + 256, 128], F32, "ExternalInput") for g in range(NG)]
    pown = [dram("pown%d" % g, [OWN, 256], F32, "ExternalInput") for g in range(NG)]
    maskd = [dram("mask%d" % g, [128, 4, 128], F32, "ExternalInput") for g in range(NG)]
    w_in = dram("w_in", [D, 2368], F32, "ExternalInput")
    w_qb = dram("w_qb", [512, 1536], F32, "ExternalInput")
    w_kvb = dram("w_kvb", [256, 2048], F32, "ExternalInput")
    w_out = dram("w_out", [D, D], F32, "ExternalInput")
    w_up = dram("w_up", [D, 8192], F32, "ExternalInput")
    w_down = dram("w_down", [8192, D], F32, "ExternalInput")
    w_gate = dram("w_gate", [D, D], F32, "ExternalInput")
    w_ple = dram("w_ple", [256, D], F32, "ExternalInput")
    gcol = dram("gcol", [128, 64], F32, "ExternalInput")
    gb3 = dram("gb3", [128, 3, D], F32, "ExternalInput")
    gsm = dram("gsm", [128, 648], F32, "ExternalInput")
    cst = dram("cst", [128, 256], F32, "ExternalInput")
    y = [dram("y%d" % g, [OWN, D], F32, "ExternalOutput") for g in range(NG)]

    wb_in = dram("wb_in", [D, 2368], BF16, "Internal")
    wb_qb = dram("wb_qb", [512, 1536], BF16, "Internal")
    wb_kvb = dram("wb_kvb", [256, 2048], BF16, "Internal")
    wb_out = dram("wb_out", [D, D], BF16, "Internal")
    wb_up = dram("wb_up", [D, 8192], BF16, "Internal")
    wb_down = dram("wb_down", [8192, D], BF16, "Internal")
    wb_gate = dram("wb_gate", [D, D], BF16, "Internal")
    wb_ple = dram("wb_ple", [256, D], BF16, "Internal")
    knT = [dram("knT%d" % g, [8, 128, S[g]], BF16, "Internal") for g in range(NG)]
    kpT = [dram("kpT%d" % g, [64, S[g]], BF16, "Internal") for g in range(NG)]
    vv = [dram("vv%d" % g, [S[g], 1024], BF16, "Internal") for g in range(NG)]
    qnT = [dram("qnT%d" % g, [8, 128, OWN], BF16, "Internal") for g in range(NG)]
    qrT = [dram("qrT%d" % g, [8, 64, OWN], BF16, "Internal") for g in range(NG)]
    oaT = [dram("oaT%d" % g, [1024, OWN], F32, "Internal") for g in range(NG)]
    B_knT = [Buf("knT%d" % g) for g in range(NG)]
    B_kpT = [Buf("kpT%d" % g) for g in range(NG)]
    B_vv = [Buf("vv%d" % g) for g in range(NG)]
    B_qnT = [Buf("qnT%d" % g) for g in range(NG)]
    B_qrT = [Buf("qrT%d" % g) for g in range(NG)]
    B_oaT = [Buf("oaT%d" % g) for g in range(NG)]
    B_y = [Buf("y%d" % g) for g in range(NG)]

    uid = [0]

    def sb(stack, name, shape, dt):
        uid[0] += 1
        return stack.enter_context(nc.sbuf_tensor("%s_u%d" % (name, uid[0]), list(shape), dt))

    def ps(stack, name, shape, dt):
        return stack.enter_context(nc.psum_tensor(name, list(shape), dt))

    bk = [ps(es, "bk%d" % i, [128, 512], F32) for i in range(6)]
    B_bk = [Buf("bk%d" % i, psum=True) for i in range(6)]
    tp = [ps(es, "tp%d" % i, [128, 1024], BF16) for i in range(2)]
    B_tp = [Buf("tp%d" % i, psum=True) for i in range(2)]
    tpc = [0]

    def next_tp():
        tpc[0] ^= 1
        return tp[tpc[0]], B_tp[tpc[0]]

    cst_sb = sb(es, "cst_sb", [128, 256], BF16); B_cst = Buf("cst")
    tr.dma("pool", cst_sb[:], cst[:, :], w=[B_cst], sem_of=B_cst)
    ident = cst_sb[:, 0:128]
    ones = cst_sb[:, 128:256]
    gcol_sb = sb(es, "gcol_sb", [128, 64], F32); B_gcol = Buf("gcol")
    tr.dma("sp", gcol_sb[:], gcol[:, :], w=[B_gcol], sem_of=B_gcol)
    gsm_sb = sb(es, "gsm_sb", [128, 648], F32); B_gsm = Buf("gsm")
    tr.dma("sp", gsm_sb[:], gsm[:, :], w=[B_gsm], sem_of=B_gsm)
    gv = sb(es, "gv", [128, 648], F32); B_gv = Buf("gv")
    SC_MLA = 192.0 ** -0.5
    SC_SWA = 128.0 ** -0.5
    tr.op("dve", lambda e: e.scalar_tensor_tensor(out=gv[:, 0:128], in0=gsm_sb[:, 0:128], scalar=SC_MLA,
                                                  in1=gsm_sb[:, 192:320], op0=ALU.mult, op1=ALU.mult), r=[B_gsm], w=[B_gv])
    tr.op("dve", lambda e: e.tensor_scalar(out=gv[:, 128:192], in0=gsm_sb[:, 128:192], scalar1=SC_MLA, scalar2=None,
                                           op0=ALU.mult), r=[B_gsm], w=[B_gv])
    tr.op("dve", lambda e: e.tensor_copy(out=gv[:, 192:256], in_=gsm_sb[:, 320:384]), r=[B_gsm], w=[B_gv])
    tr.op("dve", lambda e: e.tensor_scalar(out=gv[:, 256:384], in0=gsm_sb[:, 384:512], scalar1=SC_SWA, scalar2=None,
                                           op0=ALU.mult), r=[B_gsm], w=[B_gv])
    tr.op("dve", lambda e: e.tensor_copy(out=gv[:, 384:512], in_=gsm_sb[:, 512:640]), r=[B_gsm], w=[B_gv])
    tr.op("act", lambda e: e.activation(out=gv[:, 512:520], in_=gsm_sb[:, 640:648], func=AF.Exp), r=[B_gsm], w=[B_gv])

    B_w = {}
    for nm, src, dst in (("in", w_in, wb_in), ("qb", w_qb, wb_qb), ("kvb", w_kvb, wb_kvb), ("out", w_out, wb_out),
                         ("up", w_up, wb_up), ("down", w_down, wb_down), ("gate", w_gate, wb_gate), ("ple", w_ple, wb_ple)):
        B_w[nm] = Buf("w_" + nm)
        rows = src.shape[0]
        step = max(128, rows // 4)
        for r0 in range(0, rows, step):
            tr.dma("pool", dst[r0:r0 + step, :], src[r0:r0 + step, :], w=[B_w[nm]], sem_of=B_w[nm], acc=True)

    if _STOP[0] == 'W':
        tr.finish(list(B_w.values()))
        return nc, es
    sm = sb(es, "sm", [128, 64], F32)
    smc = [0]

    def sm_slot(n):
        if smc[0] + n > 64:
            smc[0] = 0
        a = smc[0]
        smc[0] += n
        return a

    B_sm = [Buf("sm%d" % i) for i in range(64)]

    def smb(a, n):
        return B_sm[a:a + n]

    def rstd_into(out_ap, out_bufs, in_ap, in_bufs, dim, n):
        a = sm_slot(n) if n <= 16 else None
        if a is not None:
            tmp = sm[:, a:a + n]; tb = smb(a, n)
        else:
            tmp = out_ap; tb = out_bufs
        tr.op("dve", lambda e: e.tensor_scalar(out=tmp, in0=in_ap, scalar1=1.0 / dim, scalar2=EPS, op0=ALU.mult, op1=ALU.add),
              r=in_bufs, w=tb)
        tr.op("act", lambda e: e.activation(out=tmp, in_=tmp, func=AF.Ln), r=tb, w=tb)
        tr.op("act", lambda e: e.activation(out=out_ap, in_=tmp, func=AF.Exp, scale=-0.5), r=tb, w=out_bufs)

    def rope(out16, ob, x32, xb, cos, sin, cb, half, tA, tB, tbufs):
        a = x32[:, 0:half]; b = x32[:, half:2 * half]
        tr.op("dve", lambda e: e.tensor_tensor(out=tA, in0=a, in1=cos, op=ALU.mult), r=xb + cb, w=[tbufs[0]])
        tr.op("dve", lambda e: e.tensor_tensor(out=tB, in0=b, in1=sin, op=ALU.mult), r=xb + cb, w=[tbufs[1]])
        tr.op("dve", lambda e: e.tensor_tensor(out=out16[:, 0:half], in0=tA, in1=tB, op=ALU.subtract), r=tbufs, w=ob)
        tr.op("dve", lambda e: e.tensor_tensor(out=tA, in0=b, in1=cos, op=ALU.mult), r=xb + cb, w=[tbufs[0]])
        tr.op("dve", lambda e: e.tensor_tensor(out=tB, in0=a, in1=sin, op=ALU.mult), r=xb + cb, w=[tbufs[1]])
        tr.op("dve", lambda e: e.tensor_tensor(out=out16[:, half:2 * half], in0=tA, in1=tB, op=ALU.add), r=tbufs, w=ob)

    def transposes(srcs, src_bufs, dst_sb, dst_buf, evac="dve"):
        j0 = 0
        while j0 < len(srcs):
            grp = srcs[j0:j0 + 8]
            t, tb = next_tp()
            npart = grp[0].shape[1]
            for j, s in enumerate(grp):
                tr.op("pe", lambda e, j=j, s=s: e.transpose(out=t[0:npart, j * 128:(j + 1) * 128], in_=s, identity=ident),
                      r=src_bufs + [B_cst], w=[tb], inc=(j == len(grp) - 1))
            n = len(grp) * 128
            if evac == "dve":
                tr.op("dve", lambda e: e.tensor_copy(out=dst_sb[0:npart, j0 * 128:j0 * 128 + n], in_=t[0:npart, 0:n]),
                      r=[tb], w=[dst_buf])
            else:
                tr.op("act", lambda e: e.activation(out=dst_sb[0:npart, j0 * 128:j0 * 128 + n], in_=t[0:npart, 0:n], func=AF.Copy),
                      r=[tb], w=[dst_buf])
            j0 += 8

    def wslice(wb, n0, n):
        return wb.rearrange("(c p) n -> p c n", p=128)[:, :, n0:n0 + n]

    for g in range(NG):
        SG = S[g]
        NKT = SG // 128
        NKB = SG // 512
        with ExitStack() as gs:
            rstdk = sb(gs, "rstdk%d" % g, [128, NKT * 8], F32)
            B_rstdk = Buf("rstdk%d" % g)

            with ExitStack() as a_s:
                tr.barrier()
                winA = sb(a_s, "winA", [128, DC, 832], BF16); B_winA = Buf("winA")
                tr.dma("sp", winA[:], wslice(wb_in, 0, 832), r=[B_w["in"]], w=[B_winA], sem_of=B_winA)
                for c in range(DC):
                    tr.op("dve", lambda e, c=c: e.tensor_scalar(out=winA[:, c, :], in0=winA[:, c, :], scalar1=gcol_sb[:, c:c + 1],
                                                                scalar2=None, op0=ALU.mult), r=[B_gcol, B_winA], w=[B_winA])
                wqb = sb(a_s, "wqb", [128, 4, 1536], BF16); B_wqb = Buf("wqb")
                tr.dma("sp", wqb[:], wslice(wb_qb, 0, 1536), r=[B_w["qb"]], w=[B_wqb], sem_of=B_wqb)
                for c in range(4):
                    tr.op("dve", lambda e, c=c: e.tensor_scalar(out=wqb[:, c, :], in0=wqb[:, c, :], scalar1=gcol_sb[:, 16 + c:17 + c],
                                                                scalar2=None, op0=ALU.mult), r=[B_gcol, B_wqb], w=[B_wqb])
                wkvb = sb(a_s, "wkvb", [128, 2, 2048], BF16); B_wkvb = Buf("wkvb")
                tr.dma("sp", wkvb[:], wslice(wb_kvb, 0, 2048), r=[B_w["kvb"]], w=[B_wkvb], sem_of=B_wkvb)
                for c in range(2):
                    tr.op("dve", lambda e, c=c: e.tensor_scalar(out=wkvb[:, c, :], in0=wkvb[:, c, :], scalar1=gcol_sb[:, 20 + c:21 + c],
                                                                scalar2=None, op0=ALU.mult), r=[B_gcol, B_wkvb], w=[B_wkvb])

                xb = [sb(a_s, "xb%d" % i, [128, D], BF16) for i in range(3)]; B_xb = [Buf("xb%d" % i) for i in range(3)]
                junkl = [sb(a_s, "junkA%d" % i, [128, D], BF16) for i in range(2)]; B_junkl = [Buf("junkA%d" % i) for i in range(2)]
                xT = [sb(a_s, "xT%d" % i, [128, D], BF16) for i in range(2)]; B_xT = [Buf("xT%d" % i) for i in range(2)]
                csk = [sb(a_s, "csk%d" % i, [128, 64], F32) for i in range(2)]; B_csk = [Buf("csk%d" % i) for i in range(2)]
                stl = [sb(a_s, "stA%d" % i, [128, 32], F32) for i in range(2)]
                B_stl = [[Buf("stA%d_%d" % (j, i)) for i in range(32)] for j in range(2)]
                ckvn = sb(a_s, "ckvn", [128, 256], BF16); B_ckvn = Buf("ckvn")
                kpe32 = sb(a_s, "kpe32", [128, 64], F32); B_kpe32 = Buf("kpe32")
                kr16 = sb(a_s, "kr16", [128, 64], BF16); B_kr16 = Buf("kr16")
                tA = sb(a_s, "tA", [128, 64], F32); tB = sb(a_s, "tB", [128, 64], F32); B_tAB = [Buf("tA"), Buf("tB")]
                ckvT = [sb(a_s, "ckvT%d" % i, [128, 2, 512], BF16) for i in range(2)]; B_ckvT = [Buf("ckvT%d" % i) for i in range(2)]
                kpTs = [sb(a_s, "kpTs%d" % i, [64, 512], BF16) for i in range(2)]; B_kpTs = [Buf("kpTs%d" % i) for i in range(2)]
                sq32 = sb(a_s, "sq32", [128, 1024], F32); B_sq32 = Buf("sq32")
                vst = [sb(a_s, "vst%d" % i, [128, 1024], BF16) for i in range(2)]; B_vst = [Buf("vst%d" % i) for i in range(2)]
                kst = [sb(a_s, "kst%d" % i, [128, 8, 512], BF16) for i in range(2)]; B_kst = [Buf("kst%d" % i) for i in range(2)]
                cqn = sb(a_s, "cqn", [128, 512], BF16); B_cqn = Buf("cqn")
                cqT = sb(a_s, "cqT", [128, 512], BF16); B_cqT = Buf("cqT")
                q32 = sb(a_s, "q32", [128, 1536], F32); B_q32 = Buf("q32")
                q16 = sb(a_s, "q16", [128, 1536], BF16); B_q16 = Buf("q16")
                qr32 = sb(a_s, "qr32", [128, 64], F32); B_qr32 = Buf("qr32")
                qnst = sb(a_s, "qnst", [128, 8, 512], BF16); B_qnst = Buf("qnst")
                qrst = sb(a_s, "qrst", [64, 8, 512], BF16); B_qrst = Buf("qrst")
                qtmp = sb(a_s, "qtmp", [128, 1024], BF16); B_qtmp = Buf("qtmp")

                def s1(t):
                    own = t < OT
                    blk, ti = divmod(t, 4)
                    bs = blk % 2
                    xs = t % 3
                    cs_ = t % 2
                    sl = t % 2
                    st_ = stl[sl]; B_st_ = B_stl[sl]; junk_ = junkl[sl]; B_junk_ = B_junkl[sl]
                    bkq = bk[2 * sl]; B_bkq = B_bk[2 * sl]; bkk = bk[2 * sl + 1]; B_bkk = B_bk[2 * sl + 1]
                    xTt = xT[t % 2]; B_xTt = B_xT[t % 2]
                    tr.dma("pool", xb[xs][:], xkv[g][t * 128:(t + 1) * 128, :], w=[B_xb[xs]], sem_of=B_xb[xs])
                    tr.dma("sp", csk[cs_][:], cskv[g][t * 128:(t + 1) * 128, :], w=[B_csk[cs_]], sem_of=B_csk[cs_])
                    tr.op("act", lambda e: e.activation(out=junk_[:], in_=xb[xs][:], func=AF.Square, accum_out=st_[:, 0:1]),
                          r=[B_xb[xs]], w=[B_junk_, B_st_[0]])
                    rstd_into(st_[:, 1:2], [B_st_[1]], st_[:, 0:1], [B_st_[0]], float(D), 1)
                    transposes([xb[xs][:, c * 128:(c + 1) * 128] for c in range(DC)], [B_xb[xs]], xTt, B_xTt)
                    if own:
                        for c in range(DC):
                            tr.op("pe", lambda e, c=c: e.matmul(bkq[:, 0:512], lhsT=xTt[:, c * 128:(c + 1) * 128], rhs=winA[:, c, 0:512],
                                                                start=(c == 0), stop=(c == DC - 1)),
                                  r=[B_xTt, B_winA], w=[B_bkq], inc=(c == DC - 1))
                    for c in range(DC):
                        tr.op("pe", lambda e, c=c: e.matmul(bkk[:, 0:320], lhsT=xTt[:, c * 128:(c + 1) * 128], rhs=winA[:, c, 512:832],
                                                            start=(c == 0), stop=(c == DC - 1)),
                              r=[B_xTt, B_winA], w=[B_bkk], inc=(c == DC - 1))

                def s2(t):
                    own = t < OT
                    blk, ti = divmod(t, 4)
                    bs = blk % 2
                    xs = t % 3
                    cs_ = t % 2
                    sl = t % 2
                    st_ = stl[sl]; B_st_ = B_stl[sl]; junk_ = junkl[sl]; B_junk_ = B_junkl[sl]
                    bkq = bk[2 * sl]; B_bkq = B_bk[2 * sl]; bkk = bk[2 * sl + 1]; B_bkk = B_bk[2 * sl + 1]
                    xTt = xT[t % 2]; B_xTt = B_xT[t % 2]
                    tr.op("act", lambda e: e.activation(out=junk_[:, 0:256], in_=bkk[:, 0:256], func=AF.Square, scale=st_[:, 1:2],
                                                        accum_out=st_[:, 2:3]), r=[B_bkk, B_st_[1]], w=[B_junk_, B_st_[2]])
                    rstd_into(st_[:, 3:4], [B_st_[3]], st_[:, 2:3], [B_st_[2]], 256.0, 1)
                    tr.op("dve", lambda e: e.tensor_tensor(out=st_[:, 4:5], in0=st_[:, 3:4], in1=st_[:, 1:2], op=ALU.mult),
                          r=[B_st_[3], B_st_[1]], w=[B_st_[4]])
                    tr.op("dve", lambda e: e.tensor_scalar(out=ckvn[:], in0=bkk[:, 0:256], scalar1=st_[:, 4:5], scalar2=None, op0=ALU.mult),
                          r=[B_bkk, B_st_[4]], w=[B_ckvn])
                    tr.op("act", lambda e: e.activation(out=junk_[:, 256:320], in_=bkk[:, 256:320], func=AF.Square, scale=st_[:, 1:2],
                                                        accum_out=st_[:, 5:6]), r=[B_bkk, B_st_[1]], w=[B_junk_, B_st_[5]])
                    tr.op("dve", lambda e: e.scalar_tensor_tensor(out=kpe32[:], in0=bkk[:, 256:320], scalar=st_[:, 1:2], in1=gv[:, 192:256],
                                                                  op0=ALU.mult, op1=ALU.mult), r=[B_bkk, B_st_[1], B_gv], w=[B_kpe32])
                    rope(kr16, [B_kr16], kpe32, [B_kpe32], csk[cs_][:, 0:32], csk[cs_][:, 32:64], [B_csk[cs_]], 32,
                         tA[:, 0:32], tB[:, 0:32], B_tAB)
                    t_, tb_ = next_tp()
                    for c in range(2):
                        tr.op("pe", lambda e, c=c: e.transpose(out=t_[:, c * 128:(c + 1) * 128], in_=ckvn[:, c * 128:(c + 1) * 128], identity=ident),
                              r=[B_ckvn, B_cst], w=[tb_], inc=False)
                    tr.op("pe", lambda e: e.transpose(out=t_[0:64, 256:384], in_=kr16[:], identity=ident), r=[B_kr16, B_cst], w=[tb_])
                    for c in range(2):
                        tr.op("dve", lambda e, c=c: e.tensor_copy(out=ckvT[bs][:, c, ti * 128:(ti + 1) * 128], in_=t_[:, c * 128:(c + 1) * 128]),
                              r=[tb_], w=[B_ckvT[bs]])
                    tr.op("dve", lambda e: e.tensor_copy(out=kpTs[bs][:, ti * 128:(ti + 1) * 128], in_=t_[0:64, 256:384]),
                          r=[tb_], w=[B_kpTs[bs]])
                    vs_ = t % 2
                    for hb_ in range(2):
                        for c in range(2):
                            tr.op("pe", lambda e, c=c, hb_=hb_: e.matmul(bk[4][:, :], lhsT=ckvT[bs][:, c, ti * 128:(ti + 1) * 128],
                                                                        rhs=wkvb[:, c, hb_ * 512:(hb_ + 1) * 512], start=(c == 0), stop=(c == 1)),
                                  r=[B_ckvT[bs], B_wkvb], w=[B_bk[4]], inc=(c == 1))
                        for c in range(2):
                            tr.op("pe", lambda e, c=c, hb_=hb_: e.matmul(bk[5][:, :], lhsT=ckvT[bs][:, c, ti * 128:(ti + 1) * 128],
                                                                        rhs=wkvb[:, c, 1024 + hb_ * 512:1024 + (hb_ + 1) * 512], start=(c == 0), stop=(c == 1)),
                                  r=[B_ckvT[bs], B_wkvb], w=[B_bk[5]], inc=(c == 1))
                        tr.op("act", lambda e, hb_=hb_: e.activation(out=sq32[:, hb_ * 512:(hb_ + 1) * 512], in_=bk[4][:, :], func=AF.Square),
                              r=[B_bk[4]], w=[B_sq32])
                        tr.op("act", lambda e, hb_=hb_: e.activation(out=vst[vs_][:, hb_ * 512:(hb_ + 1) * 512], in_=bk[5][:, :], func=AF.Copy),
                              r=[B_bk[5]], w=[B_vst[vs_]])
                    tr.op("dve", lambda e: e.tensor_reduce(out=st_[:, 8:16], in_=sq32[:].rearrange("p (h d) -> p h d", d=128), axis=AX.X, op=ALU.add),
                          r=[B_sq32], w=B_st_[8:16])
                    tr.op("dve", lambda e: e.tensor_scalar(out=st_[:, 16:24], in0=st_[:, 8:16], scalar1=st_[:, 5:6], scalar2=None, op0=ALU.add),
                          r=B_st_[8:16] + [B_st_[5]], w=B_st_[16:24])
                    rstd_into(rstdk[:, t * 8:(t + 1) * 8], [B_rstdk], st_[:, 16:24], B_st_[16:24], 192.0, 8)
                    tr.dma("sp", vv[g][t * 128:(t + 1) * 128, :], vst[vs_][:], r=[B_vst[vs_]], w=[B_vv[g]], sem_of=B_vst[vs_], acc=True)

                    if own:
                        tr.op("act", lambda e: e.activation(out=junk_[:, 0:512], in_=bkq[:, :], func=AF.Square, scale=st_[:, 1:2],
                                                            accum_out=st_[:, 6:7]), r=[B_bkq, B_st_[1]], w=[B_junk_, B_st_[6]])
                        rstd_into(st_[:, 7:8], [B_st_[7]], st_[:, 6:7], [B_st_[6]], 512.0, 1)
                        tr.op("dve", lambda e: e.tensor_tensor(out=st_[:, 24:25], in0=st_[:, 7:8], in1=st_[:, 1:2], op=ALU.mult),
                              r=[B_st_[7], B_st_[1]], w=[B_st_[24]])
                        tr.op("dve", lambda e: e.tensor_scalar(out=cqn[:], in0=bkq[:, :], scalar1=st_[:, 24:25], scalar2=None, op0=ALU.mult),
                              r=[B_bkq, B_st_[24]], w=[B_cqn])
                        transposes([cqn[:, c * 128:(c + 1) * 128] for c in range(4)], [B_cqn], cqT, B_cqT)
                        for hp in range(4):
                            bi = 4 + (hp % 2)
                            for c in range(4):
                                tr.op("pe", lambda e, c=c, hp=hp, bi=bi: e.matmul(bk[bi][:, 0:384], lhsT=cqT[:, c * 128:(c + 1) * 128],
                                                                                   rhs=wqb[:, c, hp * 384:(hp + 1) * 384], start=(c == 0), stop=(c == 3)),
                                      r=[B_cqT, B_wqb], w=[B_bk[bi]], inc=(c == 3))
                            for hh in range(2):
                                h = hp * 2 + hh
                                tr.op("act", lambda e, h=h, hh=hh, bi=bi: e.activation(out=junk_[:, 0:192], in_=bk[bi][:, hh * 192:(hh + 1) * 192],
                                                                                       func=AF.Square, accum_out=st_[:, 25 + hh:26 + hh]),
                                      r=[B_bk[bi]], w=[B_junk_, B_st_[25 + hh]])
                            tr.op("dve", lambda e, hp=hp, bi=bi: e.tensor_copy(out=q32[:, hp * 384:(hp + 1) * 384], in_=bk[bi][:, 0:384]),
                                  r=[B_bk[bi]], w=[B_q32])
                            rstd_into(st_[:, 27:29], B_st_[27:29], st_[:, 25:27], B_st_[25:27], 192.0, 2)
                            for hh in range(2):
                                h = hp * 2 + hh
                                tr.op("dve", lambda e, h=h, hh=hh: e.scalar_tensor_tensor(out=q16[:, h * 192:h * 192 + 128], in0=q32[:, h * 192:h * 192 + 128],
                                                                                          scalar=st_[:, 27 + hh:28 + hh], in1=gv[:, 0:128], op0=ALU.mult, op1=ALU.mult),
                                      r=[B_q32, B_st_[27 + hh], B_gv], w=[B_q16])
                                tr.op("dve", lambda e, h=h, hh=hh: e.scalar_tensor_tensor(out=qr32[:], in0=q32[:, h * 192 + 128:(h + 1) * 192],
                                                                                          scalar=st_[:, 27 + hh:28 + hh], in1=gv[:, 128:192], op0=ALU.mult, op1=ALU.mult),
                                      r=[B_q32, B_st_[27 + hh], B_gv], w=[B_qr32])
                                rope(q16[:, h * 192 + 128:(h + 1) * 192], [B_q16], qr32, [B_qr32], csk[cs_][:, 0:32], csk[cs_][:, 32:64], [B_csk[cs_]], 32,
                                     tA[:, 0:32], tB[:, 0:32], B_tAB)
                        transposes([q16[:, h * 192:h * 192 + 128] for h in range(8)], [B_q16], qtmp, B_qtmp)
                        tr.op("dve", lambda e: e.tensor_copy(out=qnst[:, :, ti * 128:(ti + 1) * 128], in_=qtmp[:].rearrange("p (h t) -> p h t", t=128)),
                              r=[B_qtmp], w=[B_qnst])
                        transposes([q16[:, h * 192 + 128:(h + 1) * 192] for h in range(8)], [B_q16], qtmp, B_qtmp)
                        tr.op("dve", lambda e: e.tensor_copy(out=qrst[:, :, ti * 128:(ti + 1) * 128], in_=qtmp[0:64, :].rearrange("p (h t) -> p h t", t=128)),
                              r=[B_qtmp], w=[B_qrst])

                    if ti == 3:
                        ks_ = blk % 2
                        for h in range(8):
                            bi = 4 + (h % 2)
                            for c in range(2):
                                tr.op("pe", lambda e, c=c, h=h, bi=bi: e.matmul(bk[bi][:, :], lhsT=wkvb[:, c, h * 128:(h + 1) * 128], rhs=ckvT[bs][:, c, :],
                                                                               start=(c == 0), stop=(c == 1)),
                                      r=[B_ckvT[bs], B_wkvb], w=[B_bk[bi]], inc=(c == 1))
                            if h % 2 == 0:
                                tr.op("act", lambda e, h=h, bi=bi: e.activation(out=kst[ks_][:, h, :], in_=bk[bi][:, :], func=AF.Copy),
                                      r=[B_bk[bi]], w=[B_kst[ks_]])
                            else:
                                tr.op("dve", lambda e, h=h, bi=bi: e.tensor_copy(out=kst[ks_][:, h, :], in_=bk[bi][:, :]),
                                      r=[B_bk[bi]], w=[B_kst[ks_]])
                        tr.dma("sp", knT[g].rearrange("h p s -> p h s")[:, :, blk * 512:(blk + 1) * 512], kst[ks_][:],
                               r=[B_kst[ks_]], w=[B_knT[g]], sem_of=B_kst[ks_], acc=True)
                        tr.dma("sp", kpT[g][:, blk * 512:(blk + 1) * 512], kpTs[bs][:], r=[B_kpTs[bs]], w=[B_kpT[g]], sem_of=B_kpTs[bs], acc=True)
                        if own:
                            tr.dma("sp", qnT[g].rearrange("h p s -> p h s")[:, :, blk * 512:(blk + 1) * 512], qnst[:],
                                   r=[B_qnst], w=[B_qnT[g]], sem_of=B_qnst, acc=True)
                            tr.dma("sp", qrT[g].rearrange("h p s -> p h s")[:, :, blk * 512:(blk + 1) * 512], qrst[:],
                                   r=[B_qrst], w=[B_qrT[g]], sem_of=B_qrst, acc=True)


                s1(0)
                for t in range(NKT):
                    if t + 1 < NKT:
                        s1(t + 1)
                    s2(t)
            if _STOP[0] == 'A':
                tr.finish([B_knT[g], B_kpT[g], B_vv[g], B_qnT[g], B_qrT[g]])
                return nc, es
            with ExitStack() as b_s:
                tr.barrier()
                NQ = min(2, NB)
                qn_sb = [sb(b_s, "qn_sb%d" % i, [128, NQ * 512], BF16) for i in range(2)]; B_qn = [Buf("qn_sb%d" % i) for i in range(2)]
                qr_sb = [sb(b_s, "qr_sb%d" % i, [128, NQ * 512], BF16) for i in range(2)]; B_qr = [Buf("qr_sb%d" % i) for i in range(2)]
                NKS = 3
                kn_sb = [sb(b_s, "kn_sb%d" % i, [128, 512], BF16) for i in range(NKS)]; B_kn = [Buf("kn_sb%d" % i) for i in range(NKS)]
                kp_sb = [sb(b_s, "kp_sb%d" % i, [128, 512], BF16) for i in range(NKS)]; B_kp = [Buf("kp_sb%d" % i) for i in range(NKS)]
                v_sb = [sb(b_s, "v_sb%d" % i, [128, 4, 128], BF16) for i in range(NKS)]; B_v = [Buf("v_sb%d" % i) for i in range(NKS)]
                for i in range(2):
                    tr.op("dve", lambda e, i=i: e.memset(qr_sb[i][64:128, :], 0.0), w=[B_qr[i]])
                for i in range(NKS):
                    tr.op("dve", lambda e, i=i: e.memset(kp_sb[i][64:128, :], 0.0), w=[B_kp[i]])
                sbank = [bk[0], bk[1], tp[0][:].bitcast(F32), tp[1][:].bitcast(F32)]
                B_sbank = [B_bk[0], B_bk[1], B_tp[0], B_tp[1]]
                NSB = 4
                LA = 3
                NPT = 8
                pT = [sb(b_s, "pT%d" % i, [128, 512], BF16) for i in range(NPT)]; B_pT = [Buf("pT%d" % i) for i in range(NPT)]
                dsA = [sb(b_s, "dsA%d" % i, [128, 512], BF16) for i in range(2 * NQ)]; B_dsA = [Buf("dsA%d" % i) for i in range(2 * NQ)]
                dsB = [sb(b_s, "dsB%d" % i, [128, 512], BF16) for i in range(NQ)]; B_dsB = [Buf("dsB%d" % i) for i in range(NQ)]
                rden = sb(b_s, "rden", [128, 512], F32); B_rden = Buf("rden")
                oast = [sb(b_s, "oast%d" % i, [128, 512], F32) for i in range(2)]; B_oast = [Buf("oast%d" % i) for i in range(2)]
                it = 0
                kbi = [0]
                osi = 0
                gidx = [0]
                for h in range(8):
                    for qp in range(0, NB, NQ):
                        qs = it % 2
                        it += 1
                        tr.dma("sp", qn_sb[qs][:], qnT[g][h, :, qp * 512:(qp + NQ) * 512], r=[B_qnT[g]], w=[B_qn[qs]], sem_of=B_qn[qs])
                        tr.dma("sp", qr_sb[qs][0:64, :], qrT[g][h, :, qp * 512:(qp + NQ) * 512], r=[B_qrT[g]], w=[B_qr[qs]], sem_of=B_qr[qs])
                        tiles = [(kb, kt, qi) for kb in range(NKB) for kt in range(4) for qi in range(NQ)]
                        kslot = {}

                        def stage1(idx):
                            kb, kt, qi = tiles[idx]
                            if kt == 0 and qi == 0:
                                ks = kbi[0] % NKS
                                kbi[0] += 1
                                kslot[kb] = ks
                                tr.dma("sp", kn_sb[ks][:], knT[g][h, :, kb * 512:(kb + 1) * 512], r=[B_knT[g]], w=[B_kn[ks]], sem_of=B_kn[ks])
                                tr.dma("sp", kp_sb[ks][0:64, :], kpT[g][:, kb * 512:(kb + 1) * 512], r=[B_kpT[g]], w=[B_kp[ks]], sem_of=B_kp[ks])
                                tr.dma("act", v_sb[ks][:], vv[g][kb * 512:(kb + 1) * 512, h * 128:(h + 1) * 128].rearrange("(j p) d -> p j d", p=128),
                                       r=[B_vv[g]], w=[B_v[ks]], sem_of=B_v[ks])
                            ks = kslot[kb]
                            gi = gidx[0] + idx
                            sbk = sbank[gi % NSB]; sbb = B_sbank[gi % NSB]
                            tr.op("pe", lambda e: e.matmul(sbk[:, :], lhsT=kn_sb[ks][:, kt * 128:(kt + 1) * 128],
                                                           rhs=qn_sb[qs][:, qi * 512:(qi + 1) * 512], start=True, stop=False),
                                  r=[B_kn[ks], B_qn[qs]], w=[sbb], inc=False)
                            tr.op("pe", lambda e: e.matmul(sbk[:, :], lhsT=kp_sb[ks][:, kt * 128:(kt + 1) * 128],
                                                           rhs=qr_sb[qs][:, qi * 512:(qi + 1) * 512], start=False, stop=True),
                                  r=[B_kp[ks], B_qr[qs]], w=[sbb])
                            ps_ = gi % NPT
                            tglob = kb * 4 + kt
                            tr.op("act", lambda e: e.activation(out=pT[ps_][:], in_=sbk[:, :], func=AF.Exp,
                                                                scale=rstdk[:, tglob * 8 + h:tglob * 8 + h + 1]),
                                  r=[sbb, B_rstdk], w=[B_pT[ps_]])

                        def stage3(idx):
                            kb, kt, qi = tiles[idx]
                            ks = kslot[kb]
                            gi = gidx[0] + idx
                            ps_ = gi % NPT
                            first = (kb == 0 and kt == 0)
                            last = (kb == NKB - 1 and kt == 3)
                            tr.op("pe", lambda e: e.matmul(bk[2 + qi][:, :], lhsT=v_sb[ks][:, kt, :], rhs=pT[ps_][:], start=first, stop=last),
                                  r=[B_v[ks], B_pT[ps_]], w=[B_bk[2 + qi]])
                            if kt in (1, 3):
                                pv_ = (gi - NQ) % NPT
                                da = dsA[(kb % 2) * NQ + qi]; dab = B_dsA[(kb % 2) * NQ + qi]
                                if kt == 1:
                                    tr.op("dve", lambda e: e.tensor_tensor(out=da[:], in0=pT[pv_][:], in1=pT[ps_][:], op=ALU.add),
                                          r=[B_pT[pv_], B_pT[ps_]], w=[dab])
                                else:
                                    tr.op("dve", lambda e: e.tensor_tensor(out=dsB[qi][:], in0=pT[pv_][:], in1=pT[ps_][:], op=ALU.add),
                                          r=[B_pT[pv_], B_pT[ps_]], w=[B_dsB[qi]])
                                    tr.op("dve", lambda e: e.tensor_tensor(out=da[:], in0=da[:], in1=dsB[qi][:], op=ALU.add),
                                          r=[dab, B_dsB[qi]], w=[dab])
                                    tr.op("pe", lambda e: e.matmul(bk[4 + qi][:, :], lhsT=ones, rhs=da[:], start=(kb == 0), stop=(kb == NKB - 1)),
                                          r=[B_cst, dab], w=[B_bk[4 + qi]])

                        for idx in range(len(tiles) + LA):
                            if idx < len(tiles):
                                stage1(idx)
                            if idx - LA >= 0:
                                stage3(idx - LA)
                        gidx[0] += len(tiles)
                        for qi in range(NQ):
                            tr.op("act", lambda e, qi=qi: e.activation(out=rden[:], in_=bk[4 + qi][:, :], func=AF.Ln), r=[B_bk[4 + qi]], w=[B_rden])
                            tr.op("act", lambda e: e.activation(out=rden[:], in_=rden[:], func=AF.Exp, scale=-1.0), r=[B_rden], w=[B_rden])
                            os_ = osi % 2
                            osi += 1
                            tr.op("dve", lambda e, qi=qi, os_=os_: e.tensor_tensor(out=oast[os_][:], in0=bk[2 + qi][:, :], in1=rden[:], op=ALU.mult),
                                  r=[B_bk[2 + qi], B_rden], w=[B_oast[os_]])
                            tr.dma("sp", oaT[g][h * 128:(h + 1) * 128, (qp + qi) * 512:(qp + qi + 1) * 512], oast[os_][:],
                                   r=[B_oast[os_]], w=[B_oaT[g]], sem_of=B_oast[os_], acc=True)

        if _STOP[0] == 'B2':
            tr.finish([B_oaT[g]])
            return nc, es
        with ExitStack() as c_s:
            tr.barrier()
            x1 = sb(c_s, "x1", [128, 4, D], F32); B_x1 = [Buf("x1_%d" % i) for i in range(4)]
            xh = sb(c_s, "xh", [128, D], F32); B_xh = Buf("xh")
            gbt = sb(c_s, "gbt", [128, D], F32); B_gbt = Buf("gbt")
            hb = sb(c_s, "hb", [128, D], BF16); B_hb = Buf("hb")
            hT = sb(c_s, "hT", [128, DC, 768], BF16); B_hT = [Buf("hT%d" % i) for i in range(6)]
            NWS = 3
            wsl = [sb(c_s, "wsl%d" % i, [128, 8192], BF16) for i in range(NWS)]; B_wsl = [Buf("wsl%d" % i) for i in range(NWS)]
            wsc = [0]

            def next_w():
                i = wsc[0] % NWS
                wsc[0] += 1
                return wsl[i], B_wsl[i]
            wpl = [sb(c_s, "wpl%d" % i, [128, 2, 512], BF16) for i in range(2)]; B_wpl = [Buf("wpl%d" % i) for i in range(2)]
            st = sb(c_s, "stC", [128, 32], F32); B_st = [Buf("stC%d" % i) for i in range(32)]
            s32 = sb(c_s, "s32", [128, 128], F32); B_s32 = Buf("s32")
            qk16 = sb(c_s, "qk16", [128, 1280], BF16); B_qk16 = Buf("qk16")
            tA = sb(c_s, "tAc", [128, 64], F32); tB = sb(c_s, "tBc", [128, 64], F32); B_tAB = [Buf("tAc"), Buf("tBc")]
            cso = [sb(c_s, "cso%d" % i, [128, 128], F32) for i in range(6)]; B_cso = [Buf("cso%d" % i) for i in range(6)]
            qsT = sb(c_s, "qsT", [128, 1024], BF16); B_qsT = Buf("qsT")
            ksT = [sb(c_s, "ksT%d" % i, [128, 256], BF16) for i in range(6)]; B_ksT = [Buf("ksT%d" % i) for i in range(6)]
            vsb = [sb(c_s, "vsb%d" % i, [128, 256], BF16) for i in range(6)]; B_vsb = [Buf("vsb%d" % i) for i in range(6)]
            msk = sb(c_s, "msk", [128, 4, 128], F32); B_msk = Buf("msk")
            tr.dma("sp", msk[:], maskd[g][:, :, :], w=[B_msk], sem_of=B_msk)
            pTs = [sb(c_s, "pTs%d" % i, [128, 512], BF16) for i in range(6)]; B_pTs = [Buf("pTs%d" % i) for i in range(6)]
            dnl = [sb(c_s, "dnl%d" % i, [128, 512], F32) for i in range(2)]; B_dnl = [Buf("dnl%d" % i) for i in range(2)]
            tpF = [tp[i][:].bitcast(F32) for i in range(2)]
            sbc = [0]
            den = sb(c_s, "den", [128, 512], F32); B_den = Buf("den")
            o32 = sb(c_s, "o32", [128, 8, 512], F32); B_o32 = Buf("o32")
            rsb = sb(c_s, "rsb", [128, 512], F32); B_rsb = Buf("rsb")
            r32 = [sb(c_s, "r32_%d" % i, [128, 512], F32) for i in range(2)]; B_r32 = [Buf("r32_%d" % i) for i in range(2)]
            aTt = sb(c_s, "aTt", [128, 8, 512], BF16); B_aT = [Buf("aT%d" % i) for i in range(2)]
            pb16 = sb(c_s, "pb16", [128, 4, 256], BF16); B_pb16 = Buf("pb16")
            ppT = sb(c_s, "ppT", [128, 2, 512], BF16); B_ppT = Buf("ppT")
            ptmp = sb(c_s, "ptmp", [128, 256], BF16); B_ptmp = Buf("ptmp")
            bkc = [0]

            def next_bk():
                i = bkc[0] % 6
                bkc[0] += 1
                return bk[i], B_bk[i]
            ptc = [0]

            def norm_T(src_ap, src_bufs, dst_col, dst_bufs):
                tr.op("act", lambda e: e.activation(out=hb[:], in_=src_ap, func=AF.Square, accum_out=st[:, 0:1]),
                      r=src_bufs, w=[B_hb, B_st[0]])
                rstd_into(st[:, 1:2], [B_st[1]], st[:, 0:1], [B_st[0]], float(D), 1)
                tr.op("dve", lambda e: e.scalar_tensor_tensor(out=hb[:], in0=src_ap, scalar=st[:, 1:2], in1=gbt[:], op0=ALU.mult, op1=ALU.mult),
                      r=src_bufs + [B_st[1], B_gbt], w=[B_hb])
                for half in range(2):
                    t_, tb_ = next_tp()
                    for j in range(8):
                        c = half * 8 + j
                        tr.op("pe", lambda e, j=j, c=c: e.transpose(out=t_[:, j * 128:(j + 1) * 128], in_=hb[:, c * 128:(c + 1) * 128], identity=ident),
                              r=[B_hb, B_cst], w=[tb_], inc=(j == 7))
                    if half == 0:
                        tr.op("dve", lambda e: e.tensor_copy(out=hT[:, 0:8, dst_col:dst_col + 128],
                                                             in_=t_[:].rearrange("p (c t) -> p c t", t=128)), r=[tb_], w=dst_bufs)
                    else:
                        tr.op("act", lambda e: e.activation(out=hT[:, 8:16, dst_col:dst_col + 128],
                                                            in_=t_[:].rearrange("p (c t) -> p c t", t=128), func=AF.Copy), r=[tb_], w=dst_bufs)

            def load_g(gi):
                tr.dma("sp", gbt[:], gb3[:, gi, :], w=[B_gbt], sem_of=B_gbt)

            def head_norm_rope(pb_, pbb, hh, go, dcol, cs_):
                tr.op("act", lambda e: e.activation(out=hb[:, 0:128], in_=pb_[:, hh * 128:(hh + 1) * 128], func=AF.Square,
                                                    accum_out=st[:, 8:9]), r=[pbb], w=[B_hb, B_st[8]])
                rstd_into(st[:, 9:10], [B_st[9]], st[:, 8:9], [B_st[8]], 128.0, 1)
                tr.op("dve", lambda e: e.scalar_tensor_tensor(out=s32[:], in0=pb_[:, hh * 128:(hh + 1) * 128], scalar=st[:, 9:10],
                                                              in1=gv[:, go:go + 128], op0=ALU.mult, op1=ALU.mult),
                      r=[pbb, B_st[9], B_gv], w=[B_s32])
                rope(qk16[:, dcol:dcol + 128], [B_qk16], s32, [B_s32], cso[cs_][:, 0:64], cso[cs_][:, 64:128], [B_cso[cs_]], 64,
                     tA[:, 0:64], tB[:, 0:64], B_tAB)

            B_s32h = [Buf("s32h%d" % i) for i in range(4)]

            def heads_norm_rope(pb_, pbb, nh, go, dcol0, j):
                n = nh * 128
                sqj = den; s32b = rsb; tAb = r32[0]; tBb = r32[1]
                tr.op("act", lambda e: e.activation(out=sqj[:, 0:n], in_=pb_[:, 0:n], func=AF.Square), r=[pbb], w=[B_den])
                tr.op("dve", lambda e: e.tensor_reduce(out=st[:, 8:8 + nh], in_=sqj[:, 0:n].rearrange("p (h d) -> p h d", d=128), axis=AX.X, op=ALU.add),
                      r=[B_den], w=B_st[8:8 + nh])
                rstd_into(st[:, 12:12 + nh], B_st[12:12 + nh], st[:, 8:8 + nh], B_st[8:8 + nh], 128.0, nh)
                for hh in range(nh):
                    tr.op("dve", lambda e, hh=hh: e.scalar_tensor_tensor(out=s32b[:, hh * 128:(hh + 1) * 128], in0=pb_[:, hh * 128:(hh + 1) * 128],
                                                                         scalar=st[:, 12 + hh:13 + hh], in1=gv[:, go:go + 128], op0=ALU.mult, op1=ALU.mult),
                          r=[pbb, B_st[12 + hh], B_gv, B_rsb], w=[B_s32h[hh]])
                x3 = s32b[:, 0:n].rearrange("p (h d) -> p h d", d=128)
                a = x3[:, :, 0:64]; b = x3[:, :, 64:128]
                cosb = cso[j][:, 0:64].unsqueeze(1).to_broadcast([128, nh, 64])
                sinb = cso[j][:, 64:128].unsqueeze(1).to_broadcast([128, nh, 64])
                o3 = qk16[:, dcol0:dcol0 + n].rearrange("p (h d) -> p h d", d=128)
                tA3 = tAb[:, 0:nh * 64].rearrange("p (h d) -> p h d", d=64)
                tB3 = tBb[:, 0:nh * 64].rearrange("p (h d) -> p h d", d=64)
                xb_ = B_s32h[0:nh] + [B_cso[j]]
                tr.op("dve", lambda e: e.tensor_tensor(out=tA3, in0=a, in1=cosb, op=ALU.mult), r=xb_, w=[B_r32[0]])
                tr.op("dve", lambda e: e.tensor_tensor(out=tB3, in0=b, in1=sinb, op=ALU.mult), r=xb_, w=[B_r32[1]])
                tr.op("dve", lambda e: e.tensor_tensor(out=o3[:, :, 0:64], in0=tA3, in1=tB3, op=ALU.subtract), r=B_r32, w=[B_qk16])
                tr.op("dve", lambda e: e.tensor_tensor(out=tA3, in0=b, in1=cosb, op=ALU.mult), r=xb_, w=[B_r32[0]])
                tr.op("dve", lambda e: e.tensor_tensor(out=tB3, in0=a, in1=sinb, op=ALU.mult), r=xb_, w=[B_r32[1]])
                tr.op("dve", lambda e: e.tensor_tensor(out=o3[:, :, 64:128], in0=tA3, in1=tB3, op=ALU.add), r=B_r32, w=[B_qk16])
                for hh in range(nh):
                    B_rsb.r.update(B_s32h[hh].r)

            def onorm(c0, gc0):
                for c in range(8):
                    tr.op("act", lambda e, c=c: e.activation(out=aTt[:, c, :], in_=o32[:, c, :], func=AF.Square), r=[B_o32], w=B_aT)
                pb_, pbb = next_bk()
                for c in range(8):
                    tr.op("pe", lambda e, c=c: e.matmul(pb_[:, :], lhsT=ones, rhs=aTt[:, c, :], start=(c == 0), stop=(c == 7)),
                          r=B_aT + [B_cst], w=[pbb], inc=(c == 7))
                rstd_into(rsb[:], [B_rsb], pb_[:, :], [pbb], 1024.0, 512)
                for c in range(8):
                    tr.op("dve", lambda e, c=c: e.scalar_tensor_tensor(out=hT[:, c0 + c, 0:512], in0=o32[:, c, :], scalar=gcol_sb[:, gc0 + c:gc0 + c + 1],
                                                                       in1=rsb[:], op0=ALU.mult, op1=ALU.mult),
                          r=[B_o32, B_gcol, B_rsb], w=B_hT[0:4])

            for b in range(NB):
                load_g(0)
                wq = []
                for n in range(3):
                    w_, wb_ = next_w()
                    wv = w_[:].rearrange("p (c n) -> p c n", n=512)
                    tr.dma("sp", wv, wslice(wb_in, 832 + n * 512, 512), r=[B_w["in"]], w=[wb_], sem_of=wb_)
                    wq.append((wv, wb_))
                for j in range(6):
                    row0 = (4 * b + j) * 128
                    if 1 <= j <= 4:
                        dst = x1[:, j - 1, :]; dbuf = B_x1[j - 1]
                    else:
                        dst = xh[:]; dbuf = B_xh
                    tr.dma("sp", dst, xown[g][row0:row0 + 128, :], w=[dbuf], sem_of=dbuf)
                    tr.dma("sp", cso[j][:], csown[g][row0:row0 + 128, :], w=[B_cso[j]], sem_of=B_cso[j])
                    norm_T(dst, [dbuf], j * 128, [B_hT[j]])
                    wv, wb_ = wq[2]
                    pb_, pbb = next_bk()
                    for c in range(DC):
                        tr.op("pe", lambda e, c=c: e.matmul(pb_[:, :], lhsT=hT[:, c, j * 128:(j + 1) * 128], rhs=wv[:, c, :],
                                                            start=(c == 0), stop=(c == DC - 1)), r=[B_hT[j], wb_], w=[pbb], inc=(c == DC - 1))
                    heads_norm_rope(pb_, pbb, 2, 384, 8 * 128, j)
                    tr.op("act", lambda e: e.activation(out=vsb[j][:], in_=pb_[:, 256:512], func=AF.Copy), r=[pbb], w=[B_vsb[j]])
                    transposes([qk16[:, (8 + hh) * 128:(9 + hh) * 128] for hh in range(2)], [B_qk16], ksT[j], B_ksT[j])
                for j in range(1, 5):
                    for n in range(2):
                        wv, wb_ = wq[n]
                        pb_, pbb = next_bk()
                        for c in range(DC):
                            tr.op("pe", lambda e, c=c: e.matmul(pb_[:, :], lhsT=hT[:, c, j * 128:(j + 1) * 128], rhs=wv[:, c, :],
                                                                start=(c == 0), stop=(c == DC - 1)), r=[B_hT[j], wb_], w=[pbb], inc=(c == DC - 1))
                        heads_norm_rope(pb_, pbb, 4, 256, n * 4 * 128, j)
                    transposes([qk16[:, h * 128:(h + 1) * 128] for h in range(8)], [B_qk16], qsT, B_qsT)
                    pis = {}
                    for kh in range(2):
                        for bi, blk in enumerate((j - 1, j, j + 1)):
                            si = sbc[0] % 4
                            sbc[0] += 1
                            sb_, sbb = bk[si], B_bk[si]
                            tr.op("pe", lambda e: e.matmul(sb_[:, :], lhsT=ksT[blk][:, kh * 128:(kh + 1) * 128], rhs=qsT[:, kh * 512:(kh + 1) * 512],
                                                           start=True, stop=True), r=[B_ksT[blk], B_qsT], w=[sbb])
                            pi = ptc[0] % 6
                            ptc[0] += 1
                            pis[(kh, bi)] = pi
                            tr.op("act", lambda e: e.activation(out=pTs[pi][:], in_=sb_[:, :], func=AF.Exp), r=[sbb], w=[B_pTs[pi]])
                            if bi != 1:
                                if bi == 0:
                                    mi = 0 if (b == 0 and j == 1) else 1
                                else:
                                    mi = 3 if (b == NB - 1 and j == 4) else 2
                                p3 = pTs[pi][:].rearrange("p (h q) -> p h q", q=128)
                                tr.op("dve", lambda e: e.tensor_tensor(out=p3, in0=p3, in1=msk[:, mi, :].unsqueeze(1).to_broadcast([128, 4, 128]), op=ALU.mult),
                                      r=[B_pTs[pi], B_msk], w=[B_pTs[pi]])
                    for kh in range(2):
                        if kh == 0:
                            ob_, obb, db_, dbb = bk[4][:, :], B_bk[4], bk[5][:, :], B_bk[5]
                        else:
                            ob_, obb, db_, dbb = tpF[0], B_tp[0], tpF[1], B_tp[1]
                        for bi, blk in enumerate((j - 1, j, j + 1)):
                            pi = pis[(kh, bi)]
                            tr.op("pe", lambda e: e.matmul(ob_, lhsT=vsb[blk][:, kh * 128:(kh + 1) * 128], rhs=pTs[pi][:], start=(bi == 0), stop=(bi == 2)),
                                  r=[B_vsb[blk], B_pTs[pi]], w=[obb], inc=False)
                            tr.op("pe", lambda e: e.matmul(db_, lhsT=ones, rhs=pTs[pi][:], start=(bi == 0), stop=(bi == 2)),
                                  r=[B_cst, B_pTs[pi]], w=[dbb])
                        dn = dnl[kh]; dnb = B_dnl[kh]
                        dn3 = dn[:].rearrange("p (h q) -> p h q", q=128)
                        tr.op("dve", lambda e: e.tensor_tensor(out=dn3, in0=db_.rearrange("p (h q) -> p h q", q=128),
                                                               in1=gv[:, 512 + 4 * kh:516 + 4 * kh].unsqueeze(2).to_broadcast([128, 4, 128]), op=ALU.add),
                              r=[dbb, B_gv], w=[dnb])
                        tr.op("act", lambda e: e.activation(out=dn[:], in_=dn[:], func=AF.Ln), r=[dnb], w=[dnb])
                        tr.op("act", lambda e: e.activation(out=dn[:], in_=dn[:], func=AF.Exp, scale=-1.0), r=[dnb], w=[dnb])
                        tr.op("dve", lambda e: e.tensor_tensor(out=o32[:, 4 * kh:4 * kh + 4, (j - 1) * 128:j * 128], in0=ob_.rearrange("p (h q) -> p h q", q=128),
                                                               in1=dn3, op=ALU.mult),
                              r=[obb, dnb], w=[B_o32])
                onorm(8, 30)
                tr.dma("sp", o32[:], oaT[g].rearrange("(c p) t -> p c t", p=128)[:, :, b * 512:(b + 1) * 512], r=[B_oaT[g]], w=[B_o32], sem_of=B_o32)
                onorm(0, 22)
                for n in range(4):
                    w_, wb_ = next_w()
                    wv = w_[:].rearrange("p (c n) -> p c n", n=512)
                    tr.dma("sp", wv, wslice(wb_out, n * 512, 512), r=[B_w["out"]], w=[wb_], sem_of=wb_)
                    for i in range(4):
                        pb_, pbb = next_bk()
                        for c in range(DC):
                            tr.op("pe", lambda e, c=c: e.matmul(pb_[:, :], lhsT=hT[:, c, i * 128:(i + 1) * 128], rhs=wv[:, c, :],
                                                                start=(c == 0), stop=(c == DC - 1)), r=B_hT[0:4] + [wb_], w=[pbb], inc=(c == DC - 1))
                        tr.op("dve", lambda e: e.tensor_tensor(out=x1[:, i, n * 512:(n + 1) * 512], in0=pb_[:, :], in1=x1[:, i, n * 512:(n + 1) * 512], op=ALU.add),
                              r=[pbb, B_x1[i]], w=[B_x1[i]])
                load_g(1)
                for i in range(4):
                    norm_T(x1[:, i, :], [B_x1[i]], i * 128, [B_hT[i]])
                for fb in range(16):
                    wu_, wub = next_w()
                    wuv = wu_[:].rearrange("p (c n) -> p c n", n=512)
                    tr.dma("sp", wuv, wslice(wb_up, fb * 512, 512), r=[B_w["up"]], w=[wub], sem_of=wub)
                    wd_, wdb = next_w()
                    wdv = wd_[:].rearrange("p (c n) -> p c n", n=2048)
                    tr.dma("act", wdv, wb_down[fb * 512:(fb + 1) * 512, :].rearrange("(c p) n -> p c n", p=128), r=[B_w["down"]], w=[wdb], sem_of=wdb)
                    asl = fb % 2
                    for fc in range(4):
                        pb_, pbb = next_bk()
                        for c in range(DC):
                            tr.op("pe", lambda e, c=c: e.matmul(pb_[:, :], lhsT=wuv[:, c, fc * 128:(fc + 1) * 128], rhs=hT[:, c, 0:512],
                                                                start=(c == 0), stop=(c == DC - 1)), r=B_hT[0:4] + [wub], w=[pbb], inc=(c == DC - 1))
                        ri = fc % 2
                        tr.op("act", lambda e: e.activation(out=r32[ri][:], in_=pb_[:, :], func=AF.Relu), r=[pbb], w=[B_r32[ri]])
                        tr.op("dve", lambda e: e.scalar_tensor_tensor(out=aTt[:, asl * 4 + fc, :], in0=pb_[:, :], scalar=0.0, in1=r32[ri][:], op0=ALU.max, op1=ALU.mult),
                              r=[pbb, B_r32[ri]], w=[B_aT[asl]])
                    for i in range(4):
                        for n in range(4):
                            pb_, pbb = next_bk()
                            for fc in range(4):
                                tr.op("pe", lambda e, fc=fc: e.matmul(pb_[:, :], lhsT=aTt[:, asl * 4 + fc, i * 128:(i + 1) * 128], rhs=wdv[:, fc, n * 512:(n + 1) * 512],
                                                                      start=(fc == 0), stop=(fc == 3)), r=[B_aT[asl], wdb], w=[pbb], inc=(fc == 3))
                            tr.op("dve", lambda e: e.tensor_tensor(out=x1[:, i, n * 512:(n + 1) * 512], in0=pb_[:, :], in1=x1[:, i, n * 512:(n + 1) * 512], op=ALU.add),
                                  r=[pbb, B_x1[i]], w=[B_x1[i]])
                load_g(2)
                for i in range(4):
                    norm_T(x1[:, i, :], [B_x1[i]], i * 128, [B_hT[i]])
                tr.dma("pool", pb16[:], pown[g][b * 512:(b + 1) * 512, :].rearrange("(i p) d -> p i d", p=128), w=[B_pb16], sem_of=B_pb16)
                for i in range(4):
                    transposes([pb16[:, i, c * 128:(c + 1) * 128] for c in range(2)], [B_pb16], ptmp, B_ptmp)
                    tr.op("dve", lambda e: e.tensor_copy(out=ppT[:, :, i * 128:(i + 1) * 128], in_=ptmp[:].rearrange("p (c t) -> p c t", t=128)),
                          r=[B_ptmp], w=[B_ppT])
                for n in range(4):
                    wg_, wgb = next_w()
                    wgv = wg_[:].rearrange("p (c n) -> p c n", n=512)
                    tr.dma("sp", wgv, wslice(wb_gate, n * 512, 512), r=[B_w["gate"]], w=[wgb], sem_of=wgb)
                    wp_ = wpl[n % 2]; wpb = B_wpl[n % 2]
                    tr.dma("act", wp_[:], wslice(wb_ple, n * 512, 512), r=[B_w["ple"]], w=[wpb], sem_of=wpb)
                    for i in range(4):
                        pg_, pgb = next_bk()
                        for c in range(DC):
                            tr.op("pe", lambda e, c=c: e.matmul(pg_[:, :], lhsT=hT[:, c, i * 128:(i + 1) * 128], rhs=wgv[:, c, :],
                                                                start=(c == 0), stop=(c == DC - 1)), r=[B_hT[i], wgb], w=[pgb], inc=(c == DC - 1))
                        pp_, ppb = next_bk()
                        for c in range(2):
                            tr.op("pe", lambda e, c=c: e.matmul(pp_[:, :], lhsT=ppT[:, c, i * 128:(i + 1) * 128], rhs=wp_[:, c, :],
                                                                start=(c == 0), stop=(c == 1)), r=[B_ppT, wpb], w=[ppb], inc=(c == 1))
                        ri = i % 2
                        tr.op("act", lambda e: e.activation(out=r32[ri][:], in_=pg_[:, :], func=AF.Exp, scale=-1.0), r=[pgb], w=[B_r32[ri]])
                        tr.op("act", lambda e: e.activation(out=r32[ri][:], in_=r32[ri][:], func=AF.Ln, bias=1.0), r=[B_r32[ri]], w=[B_r32[ri]])
                        tr.op("act", lambda e: e.activation(out=r32[ri][:], in_=r32[ri][:], func=AF.Exp, scale=-1.0), r=[B_r32[ri]], w=[B_r32[ri]])
                        tr.op("dve", lambda e: e.tensor_tensor(out=r32[ri][:], in0=pp_[:, :], in1=r32[ri][:], op=ALU.mult), r=[ppb, B_r32[ri]], w=[B_r32[ri]])
                        tr.op("dve", lambda e: e.tensor_tensor(out=x1[:, i, n * 512:(n + 1) * 512], in0=x1[:, i, n * 512:(n + 1) * 512], in1=r32[ri][:], op=ALU.add),
                              r=[B_x1[i], B_r32[ri]], w=[B_x1[i]])
                tr.dma("sp", y[g].rearrange("(i p) d -> p i d", p=128)[:, 4 * b:4 * b + 4, :], x1[:], r=B_x1, w=[B_y[g]], sem_of=B_y[g], acc=True)
                for i in range(4):
                    B_x1[i].r.update(B_y[g].w)
    tr.finish(B_y)
    return nc, es


def _rope_tab(pos, dim):
    inv = (1.0 / (10000.0 ** (np.arange(0, dim, 2, dtype=np.float32) / np.float32(dim)))).astype(np.float32)
    ang = pos.astype(np.float32)[:, None] * inv[None, :]
    return np.concatenate([np.cos(ang), np.sin(ang)], axis=1).astype(np.float32)


_CACHE = {}


def kernel(x_prompt, x_sample, p_prompt, p_sample, g_attn, w_in, g_qa, w_qb, g_kva, w_kvb,
           g_qn_mla, g_kn_mla, g_q_swa, g_k_swa, sink, g_out_mla, g_out_swa, w_out,
           g_mlp, w_up, w_down, g_ple, w_ple, w_gate):
    f = lambda a: np.ascontiguousarray(np.asarray(a, dtype=np.float32))
    x_prompt, x_sample, p_prompt, p_sample = f(x_prompt), f(x_sample), f(p_prompt), f(p_sample)
    S0 = x_prompt.shape[1]
    S1 = x_sample.shape[1]
    OWN = S0 // NCORES
    assert x_prompt.shape[0] == 1 and x_sample.shape[0] == 4 and S1 // 2 == OWN and OWN % 512 == 0
    key = (S0, S1)
    if key not in _CACHE:
        _CACHE[key] = build((S0, S1), OWN)
    nc, _es = _CACHE[key]

    rep = lambda v, n=128: np.ascontiguousarray(np.broadcast_to(f(v).reshape(1, -1), (n, f(v).size)))
    col = lambda v: np.ascontiguousarray(f(v).reshape(-1, 128).T)
    gcol = np.zeros((128, 64), np.float32)
    gcol[:, 0:16] = col(g_attn[0]); gcol[:, 16:20] = col(g_qa[0]); gcol[:, 20:22] = col(g_kva[0])
    gcol[:, 22:30] = col(g_out_mla[0]); gcol[:, 30:38] = col(g_out_swa[0])
    gb3 = np.ascontiguousarray(np.stack([rep(g_attn[0]), rep(g_mlp[0]), rep(g_ple[0])], axis=1))
    gsm = np.ascontiguousarray(np.concatenate([rep(g_qn_mla[0]), rep(g_kn_mla[0]), rep(g_q_swa[0]), rep(g_k_swa[0]), rep(sink[0])], axis=1))
    cst = np.concatenate([np.eye(128, dtype=np.float32), np.ones((128, 128), np.float32)], axis=1)
    wkvb = f(w_kvb[0]).reshape(256, 8, 2, 128).transpose(0, 2, 1, 3).reshape(256, 2048)
    wkvb = np.ascontiguousarray(wkvb)
    jj = np.arange(128)[:, None]; ii = np.arange(128)[None, :]
    m_prev = (jj >= ii).astype(np.float32); m_next = (jj <= ii).astype(np.float32); zero = np.zeros((128, 128), np.float32)
    common = dict(w_in=f(w_in[0]), w_qb=f(w_qb[0]), w_kvb=wkvb, w_out=f(w_out[0]), w_up=f(w_up[0]), w_down=f(w_down[0]),
                  w_gate=f(w_gate[0]), w_ple=f(w_ple[0]), gcol=gcol, gb3=gb3, gsm=gsm, cst=cst)
    in_maps = []
    for c in range(NCORES):
        m = dict(common)
        for g in range(2):
            if g == 0:
                xs = x_prompt[0]; ps_ = p_prompt[0, 0]; SG = S0; o0 = c * OWN
            else:
                xs = x_sample[c // 2]; ps_ = p_sample[0, c // 2]; SG = S1; o0 = (c % 2) * OWN
            perm = np.concatenate([np.arange(o0, o0 + OWN), np.arange(0, o0), np.arange(o0 + OWN, SG)])
            m["xkv%d" % g] = np.ascontiguousarray(xs[perm])
            m["cskv%d" % g] = _rope_tab(perm, 64)
            xo = np.zeros((OWN + 256, D), np.float32)
            lo, hi = max(0, o0 - 128), min(SG, o0 + OWN + 128)
            xo[lo - (o0 - 128):hi - (o0 - 128)] = xs[lo:hi]
            m["xown%d" % g] = xo
            m["csown%d" % g] = _rope_tab(np.arange(o0 - 128, o0 + OWN + 128), 128)
            m["pown%d" % g] = np.ascontiguousarray(ps_[o0:o0 + OWN])
            mk = np.stack([zero if o0 == 0 else m_prev, m_prev, m_next, zero if o0 + OWN == SG else m_next], axis=1)
            m["mask%d" % g] = np.ascontiguousarray(mk)
        in_maps.append(m)
    res = run_bass_kernel_spmd(nc, in_maps, core_ids=list(range(NCORES)))
    y_p = np.concatenate([res.results[c]["y0"] for c in range(NCORES)], axis=0)[None]
    y_s = np.stack([np.concatenate([res.results[2 * j]["y1"], res.results[2 * j + 1]["y1"]], axis=0) for j in range(4)], axis=0)
    return (y_p.astype(np.float32), y_s.astype(np.float32))
```
